# Optimizing a Trainium2 kernel written in Bass

```python
import math
import jax, jax.numpy as jnp
from jax import lax
import numpy as np

D_MODEL = 1024
BATCH = 8
SEQ = 2048
DEPTH = 4

N_MIXERS = 2
N_A_LAYERS = (DEPTH + N_MIXERS - 1) // N_MIXERS
N_B_LAYERS = DEPTH // N_MIXERS

SSM_EXPAND = 2
D_INNER = SSM_EXPAND * D_MODEL
HEAD_DIM = 64
N_HEADS = D_INNER // HEAD_DIM
N_GROUPS = 4
HEADS_PER_GROUP = N_HEADS // N_GROUPS
D_STATE = 128
SSM_CONV = 4
CONV_DIM = D_INNER + 2 * N_GROUPS * D_STATE
D_IN_PROJ = 2 * D_INNER + 2 * N_GROUPS * D_STATE + N_HEADS
CHUNK = 128
DT_MIN = 0.001
DT_MAX = 0.1

CONV_KERNEL = 31

D_FF = -(-8 * D_MODEL // (3 * 256)) * 256

EPS = 1e-5

kernel_name = "hybrid_ssd_conformer_trunk"


def rmsnorm(x, g):
    xf = x.astype(jnp.float32)
    y = xf * lax.rsqrt(jnp.mean(xf * xf, axis=-1, keepdims=True) + EPS)
    return (y * g).astype(x.dtype)


def layernorm(x, g, b):
    xf = x.astype(jnp.float32)
    mu = jnp.mean(xf, axis=-1, keepdims=True)
    xc = xf - mu
    var = jnp.mean(xc * xc, axis=-1, keepdims=True)
    return (xc * lax.rsqrt(var + EPS) * g + b).astype(x.dtype)


def gated_group_rmsnorm(y, z, g):
    h = (y * jax.nn.silu(z)).astype(jnp.float32)
    shp = h.shape
    h = h.reshape(shp[:-1] + (N_GROUPS, shp[-1] // N_GROUPS))
    h = h * lax.rsqrt(jnp.mean(h * h, axis=-1, keepdims=True) + EPS)
    return (h.reshape(shp) * g).astype(y.dtype)


def causal_depthwise_conv(x, w, b):
    k = w.shape[0]
    y = lax.conv_general_dilated(
        x, w[:, None, :], window_strides=(1,), padding=[(k - 1, 0)],
        dimension_numbers=("NWC", "WIO", "NWC"), feature_group_count=x.shape[-1])
    return y + b


def segsum(a):
    t = a.shape[-1]
    ar = jnp.broadcast_to(a[..., :, None], a.shape + (t,))
    ar = jnp.where(jnp.tril(jnp.ones((t, t), dtype=bool), -1), ar, 0.0)
    cs = jnp.cumsum(ar, axis=-2)
    return jnp.where(jnp.tril(jnp.ones((t, t), dtype=bool), 0), cs, -jnp.inf)


def ssd_chunked(x, da, bm, cm):
    b, l, h, p = x.shape
    c = l // CHUNK
    dt_ = x.dtype
    x = x.reshape(b, c, CHUNK, N_GROUPS, HEADS_PER_GROUP, p)
    bm = bm.reshape(b, c, CHUNK, N_GROUPS, D_STATE)
    cm = cm.reshape(b, c, CHUNK, N_GROUPS, D_STATE)
    a = da.reshape(b, c, CHUNK, N_GROUPS, HEADS_PER_GROUP).transpose(0, 3, 4, 1, 2)
    a_cs = jnp.cumsum(a, axis=-1)

    lmat = jnp.exp(segsum(a)).astype(dt_)
    cb = jnp.einsum("bclgn,bcsgn->bgcls", cm, bm)
    y_diag = jnp.einsum("bgrcls,bcsgrp->bclgrp", cb[:, :, None] * lmat, x)

    decay_states = jnp.exp(a_cs[..., -1:] - a_cs).astype(dt_)
    xd = x * decay_states.transpose(0, 3, 4, 1, 2)[..., None]
    states = jnp.einsum("bclgn,bclgrp->bcgrpn", bm, xd)

    chunk_tot = jnp.pad(a_cs[..., -1], ((0, 0), (0, 0), (0, 0), (1, 0)))
    decay_chunk = jnp.exp(segsum(chunk_tot))[..., :c, 1:].astype(dt_)
    states_in = jnp.einsum("bgrzj,bjgrpn->bzgrpn", decay_chunk, states)

    state_decay_out = jnp.exp(a_cs).astype(dt_).transpose(0, 3, 4, 1, 2)
    y_off = jnp.einsum("bclgn,bcgrpn->bclgrp", cm, states_in) * state_decay_out[..., None]
    return (y_diag + y_off).reshape(b, l, h, p)


def mamba2_mixer(u, w_in, conv_w, conv_b, dt_bias, a_log, d_skip, gate_norm, w_out):
    b, l, _ = u.shape
    zxbcdt = u @ w_in
    z, xbc, dt = jnp.split(zxbcdt, [D_INNER, D_INNER + CONV_DIM], axis=-1)
    xbc = jax.nn.silu(causal_depthwise_conv(xbc, conv_w, conv_b))
    xs, bm, cm = jnp.split(xbc, [D_INNER, D_INNER + N_GROUPS * D_STATE], axis=-1)
    xs = xs.reshape(b, l, N_HEADS, HEAD_DIM)
    bm = bm.reshape(b, l, N_GROUPS, D_STATE)
    cm = cm.reshape(b, l, N_GROUPS, D_STATE)
    dt = jax.nn.softplus(dt.astype(jnp.float32) + dt_bias.astype(jnp.float32))
    a = -jnp.exp(a_log.astype(jnp.float32))
    y = ssd_chunked(xs * dt[..., None].astype(xs.dtype), dt * a, bm, cm)
    y = y + xs * d_skip[:, None]
    y = gated_group_rmsnorm(y.reshape(b, l, D_INNER), z, gate_norm)
    return y @ w_out


def conformer_conv_module(u, w_pw1, b_pw1, dw_w, dw_b, ln_g, ln_b, w_pw2, b_pw2):
    h = u @ w_pw1 + b_pw1
    a, g = jnp.split(h, 2, axis=-1)
    h = a * jax.nn.sigmoid(g)
    h = causal_depthwise_conv(h, dw_w, dw_b)
    h = jax.nn.silu(layernorm(h, ln_g, ln_b))
    return h @ w_pw2 + b_pw2


def swiglu_ffn(u, w_gate, w_up, w_down):
    return (jax.nn.silu(u @ w_gate) * (u @ w_up)) @ w_down


def setup_inputs(seed: int = 0) -> dict:
    key = jax.random.key(seed)
    ks = iter(jax.random.split(key, 40))

    def nrm(shape, scale):
        return jax.random.normal(next(ks), shape, jnp.float32) * scale

    na, nb = N_A_LAYERS, N_B_LAYERS
    x = nrm((BATCH, SEQ, D_MODEL), 1.0)

    u = jax.random.uniform(next(ks), (na, N_HEADS), jnp.float32)
    dt0 = jnp.exp(u * (math.log(DT_MAX) - math.log(DT_MIN)) + math.log(DT_MIN))
    dt0 = jnp.maximum(dt0, 1e-4)
    ssm_dt_bias = dt0 + jnp.log(-jnp.expm1(-dt0))
    ssm_a_log = jnp.log(jax.random.uniform(next(ks), (na, N_HEADS), jnp.float32, 1.0, 16.0))

    return {
        "x": x,
        "ssm_norm": 1.0 + nrm((na, D_MODEL), 0.05),
        "ssm_w_in": nrm((na, D_MODEL, D_IN_PROJ), D_MODEL ** -0.5),
        "ssm_conv_w": nrm((na, SSM_CONV, CONV_DIM), SSM_CONV ** -0.5),
        "ssm_conv_b": nrm((na, CONV_DIM), 0.02),
        "ssm_dt_bias": ssm_dt_bias,
        "ssm_a_log": ssm_a_log,
        "ssm_d": 1.0 + nrm((na, N_HEADS), 0.1),
        "ssm_gate_norm": 1.0 + nrm((na, D_INNER), 0.05),
        "ssm_w_out": nrm((na, D_INNER, D_MODEL), D_INNER ** -0.5),
        "cv_norm": 1.0 + nrm((nb, D_MODEL), 0.05),
        "cv_w_pw1": nrm((nb, D_MODEL, 2 * D_MODEL), D_MODEL ** -0.5),
        "cv_b_pw1": nrm((nb, 2 * D_MODEL), 0.02),
        "cv_dw_w": nrm((nb, CONV_KERNEL, D_MODEL), CONV_KERNEL ** -0.5),
        "cv_dw_b": nrm((nb, D_MODEL), 0.02),
        "cv_ln_g": 1.0 + nrm((nb, D_MODEL), 0.05),
        "cv_ln_b": nrm((nb, D_MODEL), 0.02),
        "cv_w_pw2": nrm((nb, D_MODEL, D_MODEL), D_MODEL ** -0.5),
        "cv_b_pw2": nrm((nb, D_MODEL), 0.02),
        "ffn_norm": 1.0 + nrm((DEPTH, D_MODEL), 0.05),
        "ffn_w_gate": nrm((DEPTH, D_MODEL, D_FF), D_MODEL ** -0.5),
        "ffn_w_up": nrm((DEPTH, D_MODEL, D_FF), D_MODEL ** -0.5),
        "ffn_w_down": nrm((DEPTH, D_FF, D_MODEL), D_FF ** -0.5),
        "final_norm": 1.0 + nrm((D_MODEL,), 0.05),
    }


def reference(x, ssm_norm, ssm_w_in, ssm_conv_w, ssm_conv_b, ssm_dt_bias, ssm_a_log, ssm_d,
              ssm_gate_norm, ssm_w_out, cv_norm, cv_w_pw1, cv_b_pw1, cv_dw_w, cv_dw_b, cv_ln_g,
              cv_ln_b, cv_w_pw2, cv_b_pw2, ffn_norm, ffn_w_gate, ffn_w_up, ffn_w_down, final_norm):
    h = x
    for i in range(DEPTH):
        j = i // N_MIXERS
        if i % N_MIXERS == 0:
            h = h + mamba2_mixer(rmsnorm(h, ssm_norm[j]), ssm_w_in[j], ssm_conv_w[j], ssm_conv_b[j],
                                 ssm_dt_bias[j], ssm_a_log[j], ssm_d[j], ssm_gate_norm[j], ssm_w_out[j])
        else:
            h = h + conformer_conv_module(rmsnorm(h, cv_norm[j]), cv_w_pw1[j], cv_b_pw1[j], cv_dw_w[j],
                                          cv_dw_b[j], cv_ln_g[j], cv_ln_b[j], cv_w_pw2[j], cv_b_pw2[j])
        h = h + swiglu_ffn(rmsnorm(h, ffn_norm[i]), ffn_w_gate[i], ffn_w_up[i], ffn_w_down[i])
    return rmsnorm(h, final_norm)
```

```python
import numpy as np
from contextlib import ExitStack
import concourse.bass as bass
import concourse.mybir as mybir
from concourse.bass_utils import run_bass_kernel_spmd

F32 = mybir.dt.float32
BF16 = mybir.dt.bfloat16
AF = mybir.ActivationFunctionType
ALU = mybir.AluOpType

P = 128
D = 1024
L = 2048
NKT = 8
TB = 512
DFF = 2816
NFT = 22
DIN = 2048
NH = 32
EPS = 1e-5
DEPTH = 4


import types


def _freeze(fn):
    if fn is None or fn.__closure__ is None:
        return fn
    cells = []
    for c in fn.__closure__:
        try:
            cells.append(types.CellType(c.cell_contents))
        except ValueError:
            cells.append(c)
    return types.FunctionType(fn.__code__, fn.__globals__, fn.__name__, fn.__defaults__, tuple(cells))


class Op:
    __slots__ = ("eng", "pos", "fn", "waits", "inc", "semval", "kind")

    def __init__(self):
        self.kind = "op"


class Eng:
    def __init__(self, name, strict):
        self.name = name
        self.ops = []
        self.known = {}
        self.known_dma = {}
        self.strict = strict
        self.sem = None


class Res:
    __slots__ = ("w", "r", "dsem", "dcount", "name")

    def __init__(self, name=""):
        self.w = None
        self.r = {}
        self.dsem = None
        self.dcount = 0
        self.name = name


import os as _os
_STRICT = bool(_os.environ.get("KSTRICT"))


class Sched:
    def __init__(self):
        self.pe = Eng("pe", False)
        self.act = Eng("act", _STRICT)
        self.dve = Eng("dve", _STRICT)
        self.pool = Eng("pool", _STRICT)
        self.sp = Eng("sp", False)
        self.engs = [self.pe, self.act, self.dve, self.pool, self.sp]
        self.dma_res = []

    def _add_wait(self, eng, tok, waits):
        if tok is None:
            return
        if tok[0] == "op":
            x = tok[1]
            if x.eng is eng and not eng.strict:
                return
            k = eng.known.get(x.eng.name, -1)
            if x.pos <= k:
                return
            eng.known[x.eng.name] = x.pos
            x.inc = True
            waits.append(tok)
        else:
            _, res, val = tok
            k = eng.known_dma.get(id(res), 0)
            if val <= k:
                return
            eng.known_dma[id(res)] = val
            waits.append(tok)

    def op(self, eng, fn, reads=(), writes=()):
        o = Op()
        o.eng = eng
        o.pos = len(eng.ops)
        o.fn = _freeze(fn)
        o.inc = False
        o.semval = None
        waits = []
        for r in reads:
            self._add_wait(eng, r.w, waits)
        for w in writes:
            self._add_wait(eng, w.w, waits)
            for t in w.r.values():
                self._add_wait(eng, t, waits)
        o.waits = waits
        eng.ops.append(o)
        tok = ("op", o)
        for r in reads:
            r.r[eng.name] = tok
        for w in writes:
            w.w = tok
            w.r = {}
        return o

    def dma(self, eng, out_ap, in_ap, reads=(), writes=(), dres=None, **kw):
        assert dres is not None
        if dres.dsem is None:
            self.dma_res.append(dres)
            dres.dsem = "pending"
        dres.dcount += 1
        val = dres.dcount * 16

        def fn(e, dres=dres):
            return ("dma", e.dma_start(out=out_ap, in_=in_ap, **kw), dres)

        o = Op()
        o.kind = "dma"
        o.eng = eng
        o.pos = len(eng.ops)
        o.fn = fn
        o.inc = False
        o.semval = None
        waits = []
        for r in reads:
            self._add_wait(eng, r.w, waits)
        for w in writes:
            self._add_wait(eng, w.w, waits)
            for t in w.r.values():
                self._add_wait(eng, t, waits)
        o.waits = waits
        eng.ops.append(o)
        tok = ("dma", dres, val)
        for r in reads:
            r.r["dma" + str(id(dres))] = tok
        for w in writes:
            w.w = tok
            w.r = {}
        return o

    def barrier(self, engs=None):
        engs = engs or [self.pe, self.act, self.dve]
        lasts = []
        for e in engs:
            for o in reversed(e.ops):
                if o.kind == "op":
                    lasts.append(("op", o))
                    break
        for e in engs:
            waits = []
            for t in lasts:
                if t[1].eng is e:
                    continue
                self._add_wait(e, t, waits)
            if waits:
                o = Op()
                o.kind = "wait"
                o.eng = e
                o.pos = len(e.ops)
                o.fn = None
                o.inc = False
                o.semval = None
                o.waits = waits
                e.ops.append(o)

    def wait_all(self, eng, toks):
        waits = []
        for t in toks:
            self._add_wait(eng, t, waits)
        o = Op()
        o.kind = "wait"
        o.eng = eng
        o.pos = len(eng.ops)
        o.fn = None
        o.inc = False
        o.semval = None
        o.waits = waits
        eng.ops.append(o)

    def emit(self, nc, stack):
        for e in self.engs:
            e.sem = stack.enter_context(nc.semaphore("s_" + e.name))
            c = 0
            for o in e.ops:
                if o.inc:
                    assert o.fn is not None
                    c += 1
                o.semval = c
        for i, r in enumerate(self.dma_res):
            r.dsem = stack.enter_context(nc.semaphore("d%d" % i))
        block = stack.enter_context(nc.Block())

        def run(e, be):
            for o in e.ops:
                for t in o.waits:
                    if t[0] == "op":
                        be.wait_ge(t[1].eng.sem, t[1].semval)
                    else:
                        be.wait_ge(t[1].dsem, t[2])
                if o.fn is None:
                    continue
                ins = o.fn(be)
                if isinstance(ins, tuple):
                    ins[1].then_inc(ins[2].dsem, 16)
                    assert not o.inc
                elif o.inc:
                    ins.then_inc(e.sem, 1)

        @block.tensor
        def _(be):
            run(self.pe, be)

        @block.scalar
        def _(be):
            run(self.act, be)

        @block.vector
        def _(be):
            run(self.dve, be)

        @block.gpsimd
        def _(be):
            run(self.pool, be)

        @block.sync
        def _(be):
            run(self.sp, be)


class Arena:
    def __init__(self, handle, nf32):
        self.h = handle
        self.n = nf32
        self.off = 0

    def reset(self):
        self.off = 0

    def f32(self, n):
        a = self.h[:, self.off:self.off + n]
        self.off += n
        assert self.off <= self.n, ("arena overflow", self.off, self.n)
        return a

    def bf(self, n):
        n2 = (n + 1) // 2
        a = self.h[:, self.off:self.off + n2].bitcast(BF16)[:, 0:n]
        self.off += n2
        assert self.off <= self.n, ("arena overflow", self.off, self.n)
        return a


def col_tiles(w, kt):
    K, F = w.shape
    assert K == kt * 128 and F % 128 == 0
    a = w.reshape(kt, 128, F // 128, 128).transpose(2, 1, 0, 3)
    return np.ascontiguousarray(a).reshape(F // 128, 128, kt * 128)


def pcol(v):
    return np.ascontiguousarray(v.reshape(-1, 128).T)


class PV:
    def __init__(self):
        self.cols = []
        self.off = {}
        self.n = 0

    def add(self, name, arr):
        arr = np.asarray(arr, dtype=np.float32)
        assert arr.shape[0] == 128
        arr = arr.reshape(128, -1)
        self.off[name] = (self.n, arr.shape[1])
        self.cols.append(arr)
        self.n += arr.shape[1]

    def array(self):
        return np.ascontiguousarray(np.concatenate(self.cols, axis=1))


def pv_layout():
    pv = PV()
    z = lambda n: np.zeros((128, n), np.float32)
    for j in range(2):
        pv.add("ssm_norm%d" % j, z(8))
        pv.add("cv_norm%d" % j, z(8))
    for i in range(4):
        pv.add("ffn_norm%d" % i, z(8))
    pv.add("final_norm", z(8))
    for j in range(2):
        pv.add("ssm_conv_w%d" % j, z(96))
        pv.add("ssm_conv_b%d" % j, z(24))
        pv.add("ssm_gn%d" % j, z(16))
        pv.add("ssm_dd%d" % j, z(16))
        pv.add("cv_b1%d" % j, z(16))
        pv.add("cv_dww%d" % j, z(248))
        pv.add("cv_dwb%d" % j, z(8))
        pv.add("cv_lng%d" % j, z(8))
        pv.add("cv_lnb%d" % j, z(8))
        pv.add("cv_b2%d" % j, z(8))
    return pv.off, pv.n


def build_pv(inp):
    pv = PV()
    for j in range(2):
        pv.add("ssm_norm%d" % j, pcol(inp["ssm_norm"][j]))
        pv.add("cv_norm%d" % j, pcol(inp["cv_norm"][j]))
    for i in range(4):
        pv.add("ffn_norm%d" % i, pcol(inp["ffn_norm"][i]))
    pv.add("final_norm", pcol(inp["final_norm"]))
    for j in range(2):
        cw = inp["ssm_conv_w"][j]
        pv.add("ssm_conv_w%d" % j, np.ascontiguousarray(cw.reshape(4, 24, 128).transpose(2, 1, 0)))
        pv.add("ssm_conv_b%d" % j, pcol(inp["ssm_conv_b"][j]))
        pv.add("ssm_gn%d" % j, pcol(inp["ssm_gate_norm"][j]))
        pv.add("ssm_dd%d" % j, pcol(np.repeat(inp["ssm_d"][j], 64)))
        pv.add("cv_b1%d" % j, pcol(inp["cv_b_pw1"][j]))
        dw = inp["cv_dw_w"][j]
        pv.add("cv_dww%d" % j, np.ascontiguousarray(dw.reshape(31, 8, 128).transpose(2, 1, 0)))
        pv.add("cv_dwb%d" % j, pcol(inp["cv_dw_b"][j]))
        pv.add("cv_lng%d" % j, pcol(inp["cv_ln_g"][j]))
        pv.add("cv_lnb%d" % j, pcol(inp["cv_ln_b"][j]))
        pv.add("cv_b2%d" % j, pcol(inp["cv_b_pw2"][j]))
    return pv.array()


def build_consts():
    k = np.arange(128)
    ident = np.eye(128, dtype=np.float32)
    U = (k[:, None] > k[None, :]).astype(np.float32)
    T = (k[:, None] <= k[None, :]).astype(np.float32)
    ones = np.ones((128, 128), np.float32)
    return np.ascontiguousarray(np.concatenate([ident, U, T, ones], axis=1))


def prep_weights(inp):
    w = {}
    w["ffn_wg"] = np.stack([col_tiles(inp["ffn_w_gate"][i], 8) for i in range(4)])
    w["ffn_wu"] = np.stack([col_tiles(inp["ffn_w_up"][i], 8) for i in range(4)])
    wd = []
    for i in range(4):
        halves = [col_tiles(inp["ffn_w_down"][i][hf * 1408:(hf + 1) * 1408], 11) for hf in range(2)]
        wd.append(np.stack(halves))
    w["ffn_wd"] = np.stack(wd)
    w["ssm_win"] = np.stack([col_tiles(inp["ssm_w_in"][j][:, :5120], 8) for j in range(2)])
    wdt = []
    for j in range(2):
        a = inp["ssm_w_in"][j][:, 5120:5152]
        wdt.append(np.ascontiguousarray(a.reshape(8, 128, 32).transpose(1, 0, 2)).reshape(128, 256))
    w["ssm_wdt"] = np.stack(wdt)
    wo = []
    for j in range(2):
        wo.append(np.stack([col_tiles(inp["ssm_w_out"][j][g * 512:(g + 1) * 512], 4) for g in range(4)]))
    w["ssm_wout"] = np.stack(wo)
    w["cv_w1"] = np.stack([col_tiles(inp["cv_w_pw1"][j], 8) for j in range(2)])
    w["cv_w2"] = np.stack([col_tiles(inp["cv_w_pw2"][j], 8) for j in range(2)])
    bv = np.zeros((1, 128), np.float32)
    for j in range(2):
        bv[0, j * 64:j * 64 + 32] = inp["ssm_dt_bias"][j]
        bv[0, j * 64 + 32:j * 64 + 64] = inp["ssm_a_log"][j]
    w["bv"] = bv
    return w


WSLOT = 11 * 128
NSLOT = 5
ARENA_F32 = 27200
import os
if os.environ.get('KDEBUG'):
    ARENA_F32 = 24600


DBG_N = 2048
DBG_ITEMS = []


def build_program(layers, first, last, debug=False):
    nc = bass.Bass("TRN2", target_bir_lowering=False)
    del DBG_ITEMS[:]
    pvoff, pvn = pv_layout()
    dr = {}

    def dram(name, shape, kind="ExternalInput"):
        dr[name] = nc.dram_tensor(name, list(shape), F32, kind=kind).ap()
        return dr[name]

    xin_d = dram("hin", [L, D])
    out_d = dram("hout", [L, D], kind="ExternalOutput")
    cst_d = dram("cst", [128, 512])
    pv_d = dram("pv", [128, pvn])
    bv_d = dram("bv", [1, 128])
    need_ffn = True
    need_ssm = any(i % 2 == 0 for i in layers)
    need_cv = any(i % 2 == 1 for i in layers)
    dram("ffn_wg", [4, 22, 128, 1024]); dram("ffn_wu", [4, 22, 128, 1024]); dram("ffn_wd", [4, 2, 8, 128, 1408])
    if need_ssm:
        dram("ssm_win", [2, 40, 128, 1024]); dram("ssm_wdt", [2, 128, 256]); dram("ssm_wout", [2, 4, 8, 128, 512])
    if need_cv:
        dram("cv_w1", [2, 16, 128, 1024]); dram("cv_w2", [2, 8, 128, 1024])

    S = Sched()
    pe, act, dve, pool, sp = S.pe, S.act, S.dve, S.pool, S.sp
    stack = ExitStack()
    with stack:
        def sb(name, shape, dt):
            return stack.enter_context(nc.sbuf_tensor(name, list(shape), dt))

        h_t = sb("h", [128, NKT * L], F32)
        hv = h_t[:].rearrange("p (k t) -> p k t", k=NKT)
        cst = sb("cst_s", [128, 512], F32)
        pvs = sb("pv_s", [128, pvn], F32)
        bvs = sb("bv_s", [128, 128], F32)
        cbf = sb("cbf", [128, 256], BF16)
        wring = sb("wring", [128, NSLOT * WSLOT], BF16)
        arena_t = sb("arena", [128, ARENA_F32], F32)
        AR = Arena(arena_t, ARENA_F32)
        ps = [stack.enter_context(nc.psum_tensor("ps%d" % i, [128, 512], F32)) for i in range(8)]
        ps_res = [Res("ps%d" % i) for i in range(8)]
        ps_ctr = [0]

        def getps():
            i = ps_ctr[0] % 8
            ps_ctr[0] += 1
            return ps[i], ps_res[i]

        ident_f = cst[:, 0:128]
        U_f = cst[:, 128:256]
        T_f = cst[:, 256:384]
        ones_f = cst[:, 384:512]
        ident_b = cbf[:, 0:128]
        ones_b = cbf[:, 128:256]

        r_const = Res("const")
        r_cbf = Res("cbf")
        hres = [[Res("h%d_%d" % (k, t)) for t in range(4)] for k in range(NKT)]
        wres = [Res("w%d" % i) for i in range(NSLOT)]
        wctr = [0]

        def pcolap(name, idx):
            o, n = pvoff[name]
            assert idx < n
            return pvs[:, o + idx:o + idx + 1]

        S.dma(sp, cst[:], cst_d, writes=[r_const], dres=r_const)
        S.dma(sp, pvs[:], pv_d, writes=[r_const], dres=r_const)
        S.dma(sp, bvs[:], bv_d.partition_broadcast(128), writes=[r_const], dres=r_const)
        S.op(dve, lambda e: e.tensor_copy(out=ident_b, in_=ident_f), reads=[r_const], writes=[r_cbf])
        S.op(dve, lambda e: e.tensor_copy(out=ones_b, in_=ones_f), reads=[r_const], writes=[r_cbf])

        def wload(src_ap, nel):
            i = wctr[0] % NSLOT
            wctr[0] += 1
            dst = wring[:, i * WSLOT:i * WSLOT + nel]
            S.dma(pool, dst, src_ap, writes=[wres[i]], dres=wres[i])
            return dst, wres[i]

        def load_h():
            AR.reset()
            xin = [AR.f32(D) for _ in range(2)]
            xres = [Res("xin0"), Res("xin1")]
            for tt in range(16):
                b = tt % 2
                S.dma(sp, xin[b], xin_d[tt * 128:(tt + 1) * 128, :], writes=[xres[b]], dres=xres[b])
                for half in range(2):
                    pt, pr = getps()
                    for q in range(4):
                        kt = half * 4 + q
                        S.op(pe, lambda e, pt=pt, q=q, kt=kt, b=b: e.transpose(
                            pt[:, q * 128:(q + 1) * 128], xin[b][:, kt * 128:(kt + 1) * 128], ident_f),
                            reads=[xres[b], r_const], writes=[pr])
                    dst = hv[:, half * 4:half * 4 + 4, tt * 128:(tt + 1) * 128]
                    src = pt[:].rearrange("p (q t) -> p q t", q=4)
                    wr = [hres[half * 4 + q][tt // 4] for q in range(4)]
                    if half == 0:
                        S.op(act, lambda e, dst=dst, src=src: e.copy(out=dst, in_=src), reads=[pr], writes=wr)
                    else:
                        S.op(dve, lambda e, dst=dst, src=src: e.tensor_copy(out=dst, in_=src), reads=[pr], writes=wr)

        def rmsnorm(gname, u_ap, ures, t0, ntb, sq, sqres, rstd, rstdres):
            for tb in range(ntb):
                ts = slice(t0 + tb * TB, t0 + (tb + 1) * TB)
                htb = (t0 + tb * TB) // TB
                pt, pr = getps()
                for kt in range(NKT):
                    b = kt % 2
                    S.op(act, lambda e, b=b, kt=kt, ts=ts: e.activation(out=sq[b], in_=hv[:, kt, ts], func=AF.Square,
                                                                        scale=float(D ** -0.5)),
                         reads=[hres[kt][htb]], writes=[sqres[b]])
                    S.op(pe, lambda e, pt=pt, b=b, kt=kt: e.matmul(pt[:], lhsT=ones_b, rhs=sq[b], start=(kt == 0),
                                                                   stop=(kt == NKT - 1)),
                         reads=[sqres[b], r_cbf], writes=[pr])
                rb = tb % 2
                S.op(act, lambda e, pt=pt, rb=rb: e.activation(out=rstd[rb], in_=pt[:], func=AF.Ln, bias=eps_col),
                     reads=[pr, r_cbf], writes=[rstdres[rb]])
                S.op(act, lambda e, rb=rb: e.activation(out=rstd[rb], in_=rstd[rb], func=AF.Exp, scale=-0.5),
                     reads=[rstdres[rb]], writes=[rstdres[rb]])
                for kt in range(NKT):
                    S.op(dve, lambda e, kt=kt, ts=ts, tb=tb, rb=rb: e.scalar_tensor_tensor(
                        out=u_ap[:, kt, tb * TB:(tb + 1) * TB], in0=hv[:, kt, ts], scalar=pcolap(gname, kt),
                        in1=rstd[rb], op0=ALU.mult, op1=ALU.mult),
                        reads=[hres[kt][htb], rstdres[rb], r_const], writes=[ures[kt][tb]])

        dbgres = Res("dbg")
        dbg_off = [0]
        if debug:
            dbg_d = dram("dbg", [128, DBG_N], kind="ExternalOutput")
            dbg_stage = sb("dbg_stage", [128, DBG_N], F32)
            S.op(dve, lambda e: e.memset(dbg_stage[:], 0.0), writes=[dbgres])

        def dbg(name, ap, reads):
            if not debug:
                return
            n = ap.shape[1]
            o = dbg_off[0]
            dbg_off[0] += n
            assert dbg_off[0] <= DBG_N
            DBG_ITEMS.append((name, o, n))
            S.op(dve, lambda e: e.tensor_copy(out=dbg_stage[:, o:o + n], in_=ap), reads=list(reads), writes=[dbgres])

        eps_t = sb("eps_t", [128, 1], F32)
        eps_col = eps_t[:, 0:1]
        S.op(dve, lambda e: e.memset(eps_t[:], EPS), writes=[r_cbf])

        def ffn(i):
            S.barrier()
            AR.reset()
            u = AR.bf(NKT * L).rearrange("p (k t) -> p k t", k=NKT)
            ures = [[Res() for _ in range(4)] for _ in range(NKT)]
            actb = AR.bf(11 * L).rearrange("p (k t) -> p k t", k=11)
            ares = [[Res() for _ in range(4)] for _ in range(11)]
            sq = [AR.bf(TB) for _ in range(2)]
            sqres = [Res(), Res()]
            rstd = [AR.f32(TB) for _ in range(2)]
            rstdres = [Res(), Res()]
            sg = [AR.f32(TB) for _ in range(2)]
            sgres = [Res(), Res()]
            rmsnorm("ffn_norm%d" % i, u, ures, 0, 4, sq, sqres, rstd, rstdres)
            sgc = 0
            for half in range(2):
                for mm in range(11):
                    m = half * 11 + mm
                    wg, wgr = wload(dr["ffn_wg"][i, m], 1024)
                    wu, wur = wload(dr["ffn_wu"][i, m], 1024)
                    for tb in range(4):
                        pg, pgr = getps()
                        pu, pur = getps()
                        for kt in range(NKT):
                            S.op(pe, lambda e, pg=pg, wg=wg, kt=kt, tb=tb: e.matmul(
                                pg[:], lhsT=wg[:, kt * 128:(kt + 1) * 128], rhs=u[:, kt, tb * TB:(tb + 1) * TB],
                                start=(kt == 0), stop=(kt == NKT - 1)), reads=[wgr, ures[kt][tb]], writes=[pgr])
                        for kt in range(NKT):
                            S.op(pe, lambda e, pu=pu, wu=wu, kt=kt, tb=tb: e.matmul(
                                pu[:], lhsT=wu[:, kt * 128:(kt + 1) * 128], rhs=u[:, kt, tb * TB:(tb + 1) * TB],
                                start=(kt == 0), stop=(kt == NKT - 1)), reads=[wur, ures[kt][tb]], writes=[pur])
                        b = sgc % 2
                        sgc += 1
                        S.op(act, lambda e, b=b, pg=pg: e.activation(out=sg[b], in_=pg[:], func=AF.Silu),
                             reads=[pgr], writes=[sgres[b]])
                        S.op(dve, lambda e, b=b, pu=pu, mm=mm, tb=tb: e.tensor_tensor(
                            out=actb[:, mm, tb * TB:(tb + 1) * TB], in0=pu[:], in1=sg[b], op=ALU.mult),
                            reads=[pur, sgres[b]], writes=[ares[mm][tb]])
                for dt_ in range(NKT):
                    wd, wdr = wload(dr["ffn_wd"][i, half, dt_], 1408)
                    for tb in range(4):
                        po, por = getps()
                        for kt in range(11):
                            S.op(pe, lambda e, po=po, wd=wd, kt=kt, tb=tb: e.matmul(
                                po[:], lhsT=wd[:, kt * 128:(kt + 1) * 128], rhs=actb[:, kt, tb * TB:(tb + 1) * TB],
                                start=(kt == 0), stop=(kt == 10)), reads=[wdr, ares[kt][tb]], writes=[por])
                        S.op(dve, lambda e, po=po, dt_=dt_, tb=tb: e.tensor_tensor(
                            out=hv[:, dt_, tb * TB:(tb + 1) * TB], in0=po[:], in1=hv[:, dt_, tb * TB:(tb + 1) * TB],
                            op=ALU.add), reads=[por, hres[dt_][tb]], writes=[hres[dt_][tb]])

        def conformer(j):
            S.barrier()
            AR.reset()
            u = AR.bf(NKT * L).rearrange("p (k t) -> p k t", k=NKT)
            ures = [[Res() for _ in range(4)] for _ in range(NKT)]
            GW = 30 + L
            glu_flat = AR.bf(NKT * GW)
            glu = glu_flat.rearrange("p (k t) -> p k t", k=NKT)
            gres = [Res() for _ in range(NKT)]
            dg = [AR.bf(31 * 128).rearrange("p (k j) -> p k j", k=31) for _ in range(2)]
            dgres = [Res(), Res()]
            sq = [AR.bf(TB) for _ in range(2)]
            sqres = [Res(), Res()]
            rstd = [AR.f32(TB) for _ in range(2)]
            rstdres = [Res(), Res()]
            sg = [AR.f32(TB) for _ in range(2)]
            sgres = [Res(), Res()]
            mean = AR.f32(TB); msq = AR.f32(TB); lrstd = AR.f32(TB); nb = AR.f32(TB)
            lnres = Res()
            t1 = [AR.f32(TB) for _ in range(2)]
            t1res = [Res(), Res()]
            rmsnorm("cv_norm%d" % j, u, ures, 0, 4, sq, sqres, rstd, rstdres)
            for m in range(NKT):
                S.op(dve, lambda e, m=m: e.memset(glu[:, m, 0:30], 0.0), writes=[gres[m]])
            sgc = 0
            for m in range(NKT):
                wa, war = wload(dr["cv_w1"][j, m], 1024)
                wg, wgr = wload(dr["cv_w1"][j, m + 8], 1024)
                for tb in range(4):
                    pa, par = getps()
                    pg, pgr = getps()
                    for kt in range(NKT):
                        S.op(pe, lambda e, pa=pa, wa=wa, kt=kt, tb=tb: e.matmul(
                            pa[:], lhsT=wa[:, kt * 128:(kt + 1) * 128], rhs=u[:, kt, tb * TB:(tb + 1) * TB],
                            start=(kt == 0), stop=(kt == NKT - 1)), reads=[war, ures[kt][tb]], writes=[par])
                    for kt in range(NKT):
                        S.op(pe, lambda e, pg=pg, wg=wg, kt=kt, tb=tb: e.matmul(
                            pg[:], lhsT=wg[:, kt * 128:(kt + 1) * 128], rhs=u[:, kt, tb * TB:(tb + 1) * TB],
                            start=(kt == 0), stop=(kt == NKT - 1)), reads=[wgr, ures[kt][tb]], writes=[pgr])
                    b = sgc % 2
                    sgc += 1
                    S.op(act, lambda e, b=b, pg=pg, m=m: e.activation(out=sg[b], in_=pg[:], func=AF.Sigmoid,
                                                                      bias=pcolap("cv_b1%d" % j, 8 + m)),
                         reads=[pgr, r_const], writes=[sgres[b]])
                    S.op(dve, lambda e, b=b, pa=pa, m=m, tb=tb: e.scalar_tensor_tensor(
                        out=glu[:, m, 30 + tb * TB:30 + (tb + 1) * TB], in0=pa[:], scalar=pcolap("cv_b1%d" % j, m),
                        in1=sg[b], op0=ALU.add, op1=ALU.mult),
                        reads=[par, sgres[b], r_const], writes=[gres[m]])
            S.barrier()
            c = u
            cres = ures
            o_dw, _ = pvoff["cv_dww%d" % j]
            for m in range(NKT):
                b = m % 2
                wv = pvs[:, o_dw + m * 31:o_dw + (m + 1) * 31]
                S.op(dve, lambda e, b=b, wv=wv: e.tensor_tensor(
                    out=dg[b], in0=ident_b.unsqueeze(1).to_broadcast([128, 31, 128]),
                    in1=wv.unsqueeze(2).to_broadcast([128, 31, 128]), op=ALU.mult),
                    reads=[r_cbf, r_const], writes=[dgres[b]])
                for tb in range(4):
                    pc, pcr = getps()
                    for k in range(31):
                        S.op(pe, lambda e, pc=pc, b=b, k=k, m=m, tb=tb: e.matmul(
                            pc[:], lhsT=dg[b][:, k, :], rhs=glu[:, m, tb * TB + k:tb * TB + k + TB],
                            start=(k == 0), stop=(k == 30)), reads=[dgres[b], gres[m]], writes=[pcr])
                    S.op(act, lambda e, pc=pc, m=m, tb=tb: e.activation(
                        out=c[:, m, tb * TB:(tb + 1) * TB], in_=pc[:], func=AF.Identity,
                        bias=pcolap("cv_dwb%d" % j, m)), reads=[pcr, r_const], writes=[cres[m][tb]])
            v = glu_flat[:, 0:NKT * L].rearrange("p (k t) -> p k t", k=NKT)
            t1c = 0
            vres = [[Res() for _ in range(4)] for _ in range(NKT)]
            for tb in range(4):
                p1, p1r = getps()
                p2, p2r = getps()
                for m in range(NKT):
                    b = m % 2
                    S.op(act, lambda e, b=b, m=m, tb=tb: e.activation(out=sq[b], in_=c[:, m, tb * TB:(tb + 1) * TB],
                                                                      func=AF.Square),
                         reads=[cres[m][tb]], writes=[sqres[b]])
                    S.op(pe, lambda e, p1=p1, m=m, tb=tb: e.matmul(p1[:], lhsT=ones_b, rhs=c[:, m, tb * TB:(tb + 1) * TB],
                                                                   start=(m == 0), stop=(m == NKT - 1)),
                         reads=[cres[m][tb], r_cbf], writes=[p1r])
                    S.op(pe, lambda e, p2=p2, b=b, m=m: e.matmul(p2[:], lhsT=ones_b, rhs=sq[b], start=(m == 0),
                                                                 stop=(m == NKT - 1)),
                         reads=[sqres[b], r_cbf], writes=[p2r])
                S.op(dve, lambda e, p1=p1: e.tensor_scalar(out=mean, in0=p1[:], scalar1=1.0 / D, scalar2=None,
                                                           op0=ALU.mult), reads=[p1r], writes=[lnres])
                S.op(dve, lambda e: e.tensor_tensor(out=msq, in0=mean, in1=mean, op=ALU.mult), reads=[lnres],
                     writes=[lnres])
                S.op(dve, lambda e, p2=p2: e.scalar_tensor_tensor(out=msq, in0=p2[:], scalar=1.0 / D, in1=msq,
                                                                  op0=ALU.mult, op1=ALU.subtract),
                     reads=[p2r, lnres], writes=[lnres])
                S.op(act, lambda e: e.activation(out=lrstd, in_=msq, func=AF.Ln, bias=eps_col),
                     reads=[lnres, r_cbf], writes=[lnres])
                S.op(act, lambda e: e.activation(out=lrstd, in_=lrstd, func=AF.Exp, scale=-0.5), reads=[lnres],
                     writes=[lnres])
                S.op(dve, lambda e: e.scalar_tensor_tensor(out=nb, in0=mean, scalar=-1.0, in1=lrstd, op0=ALU.mult,
                                                           op1=ALU.mult), reads=[lnres], writes=[lnres])
                for m in range(NKT):
                    b = t1c % 2
                    t1c += 1
                    S.op(dve, lambda e, b=b, m=m, tb=tb: e.tensor_tensor(out=t1[b], in0=c[:, m, tb * TB:(tb + 1) * TB],
                                                                         in1=lrstd, op=ALU.mult),
                         reads=[cres[m][tb], lnres], writes=[t1res[b]])
                    S.op(dve, lambda e, b=b: e.tensor_tensor(out=t1[b], in0=t1[b], in1=nb, op=ALU.add),
                         reads=[t1res[b], lnres], writes=[t1res[b]])
                    S.op(act, lambda e, b=b, m=m, tb=tb: e.activation(
                        out=v[:, m, tb * TB:(tb + 1) * TB], in_=t1[b], func=AF.Silu,
                        bias=pcolap("cv_lnb%d" % j, m), scale=pcolap("cv_lng%d" % j, m)),
                        reads=[t1res[b], r_const], writes=[vres[m][tb]] + gres)
            for dt_ in range(NKT):
                w2, w2r = wload(dr["cv_w2"][j, dt_], 1024)
                for tb in range(4):
                    po, por = getps()
                    for kt in range(NKT):
                        S.op(pe, lambda e, po=po, w2=w2, kt=kt, tb=tb: e.matmul(
                            po[:], lhsT=w2[:, kt * 128:(kt + 1) * 128], rhs=v[:, kt, tb * TB:(tb + 1) * TB],
                            start=(kt == 0), stop=(kt == NKT - 1)), reads=[w2r, vres[kt][tb]], writes=[por])
                    S.op(dve, lambda e, po=po, dt_=dt_, tb=tb: e.scalar_tensor_tensor(
                        out=hv[:, dt_, tb * TB:(tb + 1) * TB], in0=po[:], scalar=pcolap("cv_b2%d" % j, dt_),
                        in1=hv[:, dt_, tb * TB:(tb + 1) * TB], op0=ALU.add, op1=ALU.add),
                        reads=[por, hres[dt_][tb], r_const], writes=[hres[dt_][tb]])

        def mamba(j):
            o_cw, _ = pvoff["ssm_conv_w%d" % j]
            HT = 1024
            for half in range(2):
                S.barrier()
                AR.reset()
                T0 = half * HT
                u = AR.bf(NKT * HT).rearrange("p (k t) -> p k t", k=NKT)
                ures = [[Res() for _ in range(2)] for _ in range(NKT)]
                sq = [AR.bf(TB) for _ in range(2)]
                sqres = [Res(), Res()]
                rstd = [AR.f32(TB) for _ in range(2)]
                rstdres = [Res(), Res()]
                rmsnorm("ssm_norm%d" % j, u, ures, T0, 2, sq, sqres, rstd, rstdres)
                if half == 0 and j == 0:
                    dbg("u", u[:, 0, 0:128], [ures[0][0]])
                wdt = wdt_t[:, :]
                S.dma(pool, wdt, dr["ssm_wdt"][j], writes=[wdtr], dres=wdtr)
                tdt = AR.f32(256).rearrange("p (c h) -> p c h", c=8)
                dtv = AR.f32(256).rearrange("p (c h) -> p c h", c=8)
                da = AR.f32(256).rearrange("p (c h) -> p c h", c=8)
                dec = AR.f32(256).rearrange("p (c h) -> p c h", c=8)
                eat = AR.f32(256).rearrange("p (c h) -> p c h", c=8)
                dtdec = AR.f32(256).rearrange("p (c h) -> p c h", c=8)
                expal = AR.f32(32)
                dres = Res()
                pd, pdr = getps()
                pdv = pd[:, 0:256].rearrange("p (c h) -> p c h", c=8)
                for c in range(8):
                    for kt in range(NKT):
                        S.op(pe, lambda e, c=c, kt=kt: e.matmul(
                            pd[:, c * 32:(c + 1) * 32], lhsT=u[:, kt, c * 128:(c + 1) * 128],
                            rhs=wdt[:, kt * 32:(kt + 1) * 32], start=(kt == 0), stop=(kt == NKT - 1)),
                            reads=[ures[kt][c // 4], wdtr], writes=[pdr])
                dtb = bvs[:, j * 64:j * 64 + 32]
                alog = bvs[:, j * 64 + 32:j * 64 + 64]
                S.op(dve, lambda e: e.tensor_tensor(out=tdt, in0=pdv, in1=dtb.unsqueeze(1).to_broadcast([128, 8, 32]),
                                                    op=ALU.add), reads=[pdr, r_const], writes=[dres])
                S.op(act, lambda e: e.activation(out=tdt, in_=tdt, func=AF.Exp), reads=[dres], writes=[dres])
                S.op(act, lambda e: e.activation(out=dtv, in_=tdt, func=AF.Ln, bias=1.0), reads=[dres], writes=[dres])
                S.op(act, lambda e: e.activation(out=expal, in_=alog, func=AF.Exp), reads=[r_const], writes=[dres])
                S.op(dve, lambda e: e.scalar_tensor_tensor(
                    out=da, in0=dtv, scalar=-1.0, in1=expal.unsqueeze(1).to_broadcast([128, 8, 32]),
                    op0=ALU.mult, op1=ALU.mult), reads=[dres], writes=[dres])
                pq, pqr = getps()
                for c in range(8):
                    S.op(pe, lambda e, c=c: e.matmul(pq[:, c * 32:(c + 1) * 32], lhsT=U_f, rhs=da[:, c, :], start=True,
                                                     stop=True), reads=[dres, r_const], writes=[pqr])
                for c in range(8):
                    S.op(pe, lambda e, c=c: e.matmul(pq[:, 256 + c * 32:256 + (c + 1) * 32], lhsT=ones_f, rhs=da[:, c, :],
                                                     start=True, stop=True), reads=[dres, r_const], writes=[pqr])
                S.op(act, lambda e: e.activation(out=dec, in_=pq[:, 0:256].rearrange("p (c h) -> p c h", c=8),
                                                 func=AF.Exp), reads=[pqr], writes=[dres])
                S.op(act, lambda e: e.activation(out=eat, in_=pq[:, 256:512].rearrange("p (c h) -> p c h", c=8),
                                                 func=AF.Exp), reads=[pqr], writes=[dres])
                S.op(dve, lambda e: e.tensor_tensor(out=dtdec, in0=dtv, in1=dec, op=ALU.mult), reads=[dres],
                     writes=[dres])
                if half == 0 and j == 0:
                    dbg("dtv", dtv[:, 0, :], [dres])
                    dbg("da", da[:, 0, :], [dres])
                    dbg("dec", dec[:, 0, :], [dres])
                zs = AR.bf(4 * HT).rearrange("p (k t) -> p k t", k=4)
                zres = [[Res() for _ in range(8)] for _ in range(4)]
                xs = AR.bf(4 * HT).rearrange("p (k t) -> p k t", k=4)
                xres = [Res() for _ in range(4)]
                BT = AR.bf(HT); CT = AR.bf(HT)
                bres = Res(); cres_ = Res()
                pre = [AR.bf(4 + HT) for _ in range(2)]
                preres = [Res(), Res()]
                cdg = [AR.bf(4 * 128).rearrange("p (k j) -> p k j", k=4) for _ in range(2)]
                cdgres = [Res(), Res()]
                Rb = AR.f32(1024)
                Rres = Res()
                LT = [AR.bf(1024) for _ in range(2)]; LTres = [Res(), Res()]
                Eb = [AR.bf(1024) for _ in range(2)]; Ebres = [Res(), Res()]
                MT = [AR.bf(1024) for _ in range(2)]; MTres = [Res(), Res()]
                CTe = [AR.bf(1024) for _ in range(2)]; CTeres = [Res(), Res()]
                CBm = [AR.f32(128) for _ in range(2)]; CBmres = [Res(), Res()]
                Xpad = [AR.bf(1024) for _ in range(2)]; Xpres = [Res(), Res()]
                Xdec = [AR.bf(512) for _ in range(2)]; Xdres = [Res(), Res()]
                Btm = [AR.bf(128) for _ in range(2)]; Btres = [Res(), Res()]
                Spad = [AR.bf(1024) for _ in range(2)]; Spres = [Res(), Res()]
                Stmp = AR.f32(512); Stres = Res()
                yf = [AR.f32(512) for _ in range(2)]; yfres = [Res(), Res()]
                ysq = [AR.bf(512) for _ in range(2)]; ysqres = [Res(), Res()]
                grs = [AR.f32(128) for _ in range(2)]; grsres = [Res(), Res()]
                for b in range(2):
                    S.op(dve, lambda e, b=b: e.memset(Xpad[b], 0.0), writes=[Xpres[b]])
                    S.op(dve, lambda e, b=b: e.memset(Spad[b], 0.0), writes=[Spres[b]])
                cvc = 0
                for g in range(4):
                    Sst = ssm_state[g]
                    Ssr = ssm_state_res[g]
                    tiles = [(32 + g, "B", 0), (36 + g, "C", 0)] + [(16 + 4 * g + q, "x", q) for q in range(4)]
                    for ti, (wt, kind, q) in enumerate(tiles):
                        ctile = {"x": 4 * g + q, "B": 16 + g, "C": 20 + g}[kind]
                        w_, wr_ = wload(dr["ssm_win"][j, wt], 1024)
                        b = cvc % 2
                        cvc += 1
                        tl = tails[:, (g * 6 + ti) * 4:(g * 6 + ti) * 4 + 3]
                        if half == 0:
                            S.op(dve, lambda e, b=b: e.memset(pre[b][:, 0:3], 0.0), writes=[preres[b]])
                        else:
                            S.op(dve, lambda e, b=b, tl=tl: e.tensor_copy(out=pre[b][:, 0:3], in_=tl),
                                 reads=[tailres], writes=[preres[b]])
                        for tbh in range(2):
                            pp, ppr = getps()
                            for kt in range(NKT):
                                S.op(pe, lambda e, pp=pp, w_=w_, kt=kt, tbh=tbh: e.matmul(
                                    pp[:], lhsT=w_[:, kt * 128:(kt + 1) * 128], rhs=u[:, kt, tbh * TB:(tbh + 1) * TB],
                                    start=(kt == 0), stop=(kt == NKT - 1)), reads=[wr_, ures[kt][tbh]], writes=[ppr])
                            S.op(act, lambda e, pp=pp, b=b, tbh=tbh: e.copy(out=pre[b][:, 3 + tbh * TB:3 + (tbh + 1) * TB],
                                                                           in_=pp[:]), reads=[ppr], writes=[preres[b]])
                        if half == 0:
                            S.op(dve, lambda e, b=b, tl=tl: e.tensor_copy(out=tl, in_=pre[b][:, HT:HT + 3]),
                                 reads=[preres[b]], writes=[tailres])
                        wv = pvs[:, o_cw + ctile * 4:o_cw + ctile * 4 + 4]
                        S.op(dve, lambda e, b=b, wv=wv: e.tensor_tensor(
                            out=cdg[b], in0=ident_b.unsqueeze(1).to_broadcast([128, 4, 128]),
                            in1=wv.unsqueeze(2).to_broadcast([128, 4, 128]), op=ALU.mult),
                            reads=[r_cbf, r_const], writes=[cdgres[b]])
                        if kind == "B":
                            dst, dstres = BT, [bres]
                        elif kind == "C":
                            dst, dstres = CT, [cres_]
                        else:
                            dst, dstres = xs[:, q, :], [xres[q]]
                        for tbh in range(2):
                            pc, pcr = getps()
                            for k in range(4):
                                S.op(pe, lambda e, pc=pc, b=b, k=k, tbh=tbh: e.matmul(
                                    pc[:], lhsT=cdg[b][:, k, :], rhs=pre[b][:, tbh * TB + k:tbh * TB + k + TB],
                                    start=(k == 0), stop=(k == 3)), reads=[cdgres[b], preres[b]], writes=[pcr])
                            S.op(act, lambda e, pc=pc, dst=dst, tbh=tbh, ctile=ctile: e.activation(
                                out=dst[:, tbh * TB:(tbh + 1) * TB], in_=pc[:], func=AF.Silu,
                                bias=pcolap("ssm_conv_b%d" % j, ctile)), reads=[pcr, r_const], writes=dstres)
                    for q in range(4):
                        w_, wr_ = wload(dr["ssm_win"][j, 4 * g + q], 1024)
                        for tbh in range(2):
                            pp, ppr = getps()
                            for kt in range(NKT):
                                S.op(pe, lambda e, pp=pp, w_=w_, kt=kt, tbh=tbh: e.matmul(
                                    pp[:], lhsT=w_[:, kt * 128:(kt + 1) * 128], rhs=u[:, kt, tbh * TB:(tbh + 1) * TB],
                                    start=(kt == 0), stop=(kt == NKT - 1)), reads=[wr_, ures[kt][tbh]], writes=[ppr])
                            S.op(act, lambda e, pp=pp, q=q, tbh=tbh: e.activation(
                                out=zs[:, q, tbh * TB:(tbh + 1) * TB], in_=pp[:], func=AF.Silu),
                                reads=[ppr], writes=zres[q][tbh * 4:(tbh + 1) * 4])
                    if half == 0 and j == 0 and g == 0:
                        dbg("BT", BT[:, 0:128], [bres])
                        dbg("CT", CT[:, 0:128], [cres_])
                        dbg("xs", xs[:, 0, 0:128], [xres[0]])
                        dbg("zs", zs[:, 0, 0:128], [zres[0][0]])
                    if half == 1:
                        spv0 = Spad[0].rearrange("p (a r) -> p a r", r=256)
                        ssv0 = Sst.rearrange("p (a r) -> p a r", r=128)
                        for par in range(2):
                            S.op(act, lambda e, par=par, spv0=spv0, ssv0=ssv0: e.copy(
                                out=spv0[:, :, par * 192:par * 192 + 64], in_=ssv0[:, :, par * 64:par * 64 + 64]),
                                reads=[Ssr], writes=[Spres[0]])
                    for c in range(8):
                        b = c % 2
                        cs = slice(c * 128, (c + 1) * 128)
                        first_chunk = (half == 0 and c == 0)
                        px, pxr = getps()
                        for q in range(4):
                            S.op(pe, lambda e, px=px, q=q, cs=cs: e.matmul(
                                px[:, q * 128:(q + 1) * 128], lhsT=xs[:, q, cs], rhs=ident_b, start=True, stop=True),
                                reads=[xres[q], r_cbf], writes=[pxr])
                        pxv = px[:].rearrange("p (a r) -> p a r", r=128)
                        xpv = Xpad[b].rearrange("p (a r) -> p a r", r=256)
                        dtg = dtv[:, c, 8 * g:8 * g + 8].rearrange("p (a r) -> p a r", r=2)
                        for par in range(2):
                            S.op(dve, lambda e, par=par, pxv=pxv, xpv=xpv, dtg=dtg: e.tensor_tensor(
                                out=xpv[:, :, par * 192:par * 192 + 64], in0=pxv[:, :, par * 64:par * 64 + 64],
                                in1=dtg[:, :, par:par + 1].to_broadcast([128, 4, 64]), op=ALU.mult),
                                reads=[pxr, dres], writes=[Xpres[b]])
                        S.op(dve, lambda e, px=px, b=b, c=c: e.tensor_tensor(
                            out=Xdec[b].rearrange("p (h q) -> p h q", h=8), in0=px[:].rearrange("p (h q) -> p h q", h=8),
                            in1=dtdec[:, c, 8 * g:8 * g + 8].unsqueeze(2).to_broadcast([128, 8, 64]), op=ALU.mult),
                            reads=[pxr, dres], writes=[Xdres[b]])
                        pb, pbr = getps()
                        S.op(pe, lambda e, pb=pb, cs=cs: e.matmul(pb[:, 0:128], lhsT=BT[:, cs], rhs=ident_b, start=True,
                                                                  stop=True), reads=[bres, r_cbf], writes=[pbr])
                        S.op(pe, lambda e, pb=pb, cs=cs: e.matmul(pb[:, 128:256], lhsT=BT[:, cs], rhs=CT[:, cs], start=True,
                                                                  stop=True), reads=[bres, cres_], writes=[pbr])
                        S.op(act, lambda e, pb=pb, b=b: e.copy(out=Btm[b], in_=pb[:, 0:128]), reads=[pbr],
                             writes=[Btres[b]])
                        S.op(dve, lambda e, pb=pb, b=b: e.tensor_tensor(out=CBm[b], in0=pb[:, 128:256], in1=T_f,
                                                                        op=ALU.mult), reads=[pbr, r_const],
                             writes=[CBmres[b]])
                        S.op(dve, lambda e, c=c: e.tensor_tensor(
                            out=Rb.rearrange("p (h l) -> p h l", h=8),
                            in0=da[:, c, 8 * g:8 * g + 8].unsqueeze(2).to_broadcast([128, 8, 128]),
                            in1=T_f.unsqueeze(1).to_broadcast([128, 8, 128]), op=ALU.mult),
                            reads=[dres, r_const], writes=[Rres])
                        pl0, pl0r = getps()
                        pl1, pl1r = getps()
                        S.op(pe, lambda e, pl0=pl0: e.matmul(pl0[:], lhsT=U_f, rhs=Rb[:, 0:512], start=True, stop=True),
                             reads=[Rres, r_const], writes=[pl0r])
                        S.op(pe, lambda e, pl1=pl1: e.matmul(pl1[:], lhsT=U_f, rhs=Rb[:, 512:1024], start=True, stop=True),
                             reads=[Rres, r_const], writes=[pl1r])
                        S.op(act, lambda e, pl0=pl0, b=b: e.activation(out=LT[b][:, 0:512], in_=pl0[:], func=AF.Exp),
                             reads=[pl0r], writes=[LTres[b]])
                        S.op(act, lambda e, pl1=pl1, b=b: e.activation(out=LT[b][:, 512:1024], in_=pl1[:], func=AF.Exp),
                             reads=[pl1r], writes=[LTres[b]])
                        S.op(dve, lambda e, b=b: e.tensor_tensor(
                            out=MT[b].rearrange("p (h l) -> p h l", h=8), in0=LT[b].rearrange("p (h l) -> p h l", h=8),
                            in1=CBm[b].unsqueeze(1).to_broadcast([128, 8, 128]), op=ALU.mult),
                            reads=[LTres[b], CBmres[b]], writes=[MTres[b]])
                        if first_chunk and j == 0 and g == 0:
                            dbg("CBm", CBm[b], [CBmres[b]])
                            dbg("LT", LT[b][:, 0:128], [LTres[b]])
                            dbg("MT", MT[b][:, 0:128], [MTres[b]])
                            dbg("Xpad", Xpad[b][:, 0:128], [Xpres[b]])
                        if not first_chunk:
                            pe0, pe0r = getps()
                            pe1, pe1r = getps()
                            S.op(pe, lambda e, pe0=pe0: e.matmul(pe0[:], lhsT=ones_f, rhs=Rb[:, 0:512], start=True,
                                                                 stop=True), reads=[Rres, r_const], writes=[pe0r])
                            S.op(pe, lambda e, pe1=pe1: e.matmul(pe1[:], lhsT=ones_f, rhs=Rb[:, 512:1024], start=True,
                                                                 stop=True), reads=[Rres, r_const], writes=[pe1r])
                            S.op(act, lambda e, pe0=pe0, b=b: e.activation(out=Eb[b][:, 0:512], in_=pe0[:], func=AF.Exp),
                                 reads=[pe0r], writes=[Ebres[b]])
                            S.op(act, lambda e, pe1=pe1, b=b: e.activation(out=Eb[b][:, 512:1024], in_=pe1[:],
                                                                           func=AF.Exp), reads=[pe1r], writes=[Ebres[b]])
                            S.op(dve, lambda e, b=b, cs=cs: e.tensor_tensor(
                                out=CTe[b].rearrange("p (h l) -> p h l", h=8),
                                in0=Eb[b].rearrange("p (h l) -> p h l", h=8),
                                in1=CT[:, cs].unsqueeze(1).to_broadcast([128, 8, 128]), op=ALU.mult),
                                reads=[Ebres[b], cres_], writes=[CTeres[b]])
                        py, pyr = getps()
                        for a in range(4):
                            nmm = 2 if first_chunk else 4
                            idx = 0
                            for par in range(2):
                                hh = 2 * a + par
                                S.op(pe, lambda e, py=py, a=a, hh=hh, b=b, idx=idx, nmm=nmm: e.matmul(
                                    py[:, a * 128:(a + 1) * 128], lhsT=Xpad[b][:, hh * 128:(hh + 1) * 128],
                                    rhs=MT[b][:, hh * 128:(hh + 1) * 128], start=(idx == 0), stop=(idx == nmm - 1)),
                                    reads=[Xpres[b], MTres[b]], writes=[pyr])
                                idx += 1
                                if not first_chunk:
                                    S.op(pe, lambda e, py=py, a=a, hh=hh, b=b, idx=idx, nmm=nmm: e.matmul(
                                        py[:, a * 128:(a + 1) * 128], lhsT=Spad[b][:, hh * 128:(hh + 1) * 128],
                                        rhs=CTe[b][:, hh * 128:(hh + 1) * 128], start=(idx == 0), stop=(idx == nmm - 1)),
                                        reads=[Spres[b], CTeres[b]], writes=[pyr])
                                    idx += 1
                        pst, pstr = getps()
                        S.op(pe, lambda e, pst=pst, b=b: e.matmul(pst[:], lhsT=Btm[b], rhs=Xdec[b], start=True, stop=True),
                             reads=[Btres[b], Xdres[b]], writes=[pstr])
                        if first_chunk:
                            S.op(dve, lambda e, pst=pst, Sst=Sst: e.tensor_copy(out=Sst, in_=pst[:]), reads=[pstr],
                                 writes=[Ssr])
                        else:
                            S.op(dve, lambda e, Sst=Sst, c=c: e.tensor_tensor(
                                out=Stmp.rearrange("p (h q) -> p h q", h=8), in0=Sst.rearrange("p (h q) -> p h q", h=8),
                                in1=eat[:, c, 8 * g:8 * g + 8].unsqueeze(2).to_broadcast([128, 8, 64]), op=ALU.mult),
                                reads=[Ssr, dres], writes=[Stres])
                            S.op(dve, lambda e, pst=pst, Sst=Sst: e.tensor_tensor(out=Sst, in0=pst[:], in1=Stmp, op=ALU.add),
                                 reads=[pstr, Stres], writes=[Ssr])
                        nb_ = (c + 1) % 2
                        spv = Spad[nb_].rearrange("p (a r) -> p a r", r=256)
                        ssv = Sst.rearrange("p (a r) -> p a r", r=128)
                        for par in range(2):
                            S.op(act, lambda e, par=par, spv=spv, ssv=ssv: e.copy(
                                out=spv[:, :, par * 192:par * 192 + 64], in_=ssv[:, :, par * 64:par * 64 + 64]),
                                reads=[Ssr], writes=[Spres[nb_]])
                        for a in range(4):
                            S.op(dve, lambda e, a=a, py=py, b=b, cs=cs: e.scalar_tensor_tensor(
                                out=yf[b][:, a * 128:(a + 1) * 128], in0=xs[:, a, cs],
                                scalar=pcolap("ssm_dd%d" % j, 4 * g + a), in1=py[:, a * 128:(a + 1) * 128],
                                op0=ALU.mult, op1=ALU.add), reads=[xres[a], pyr, r_const], writes=[yfres[b]])
                        S.op(dve, lambda e, b=b, cs=cs: e.tensor_tensor(
                            out=yf[b].rearrange("p (a l) -> p a l", a=4), in0=yf[b].rearrange("p (a l) -> p a l", a=4),
                            in1=zs[:, :, cs], op=ALU.mult), reads=[yfres[b]] + [zres[a][c] for a in range(4)],
                            writes=[yfres[b]])
                        if first_chunk and j == 0 and g == 0:
                            dbg("yg", yf[b][:, 0:128], [yfres[b]])
                        S.op(act, lambda e, b=b: e.activation(out=ysq[b], in_=yf[b], func=AF.Square,
                                                              scale=float(512 ** -0.5)), reads=[yfres[b]],
                             writes=[ysqres[b]])
                        pn, pnr = getps()
                        for a in range(4):
                            S.op(pe, lambda e, pn=pn, a=a, b=b: e.matmul(pn[:, 0:128], lhsT=ones_b,
                                                                         rhs=ysq[b][:, a * 128:(a + 1) * 128],
                                                                         start=(a == 0), stop=(a == 3)),
                                 reads=[ysqres[b], r_cbf], writes=[pnr])
                        S.op(act, lambda e, pn=pn, b=b: e.activation(out=grs[b], in_=pn[:, 0:128], func=AF.Ln,
                                                                     bias=eps_col),
                             reads=[pnr, r_cbf], writes=[grsres[b]])
                        S.op(act, lambda e, b=b: e.activation(out=grs[b], in_=grs[b], func=AF.Exp, scale=-0.5),
                             reads=[grsres[b]], writes=[grsres[b]])
                        for a in range(4):
                            S.op(dve, lambda e, a=a, b=b, cs=cs: e.scalar_tensor_tensor(
                                out=zs[:, a, cs], in0=yf[b][:, a * 128:(a + 1) * 128],
                                scalar=pcolap("ssm_gn%d" % j, 4 * g + a), in1=grs[b], op0=ALU.mult, op1=ALU.mult),
                                reads=[yfres[b], grsres[b], r_const], writes=[zres[a][c]])
                    if half == 0 and j == 0 and g == 0:
                        dbg("yn", zs[:, 0, 0:128], [zres[0][0]])
                    for dt_ in range(NKT):
                        wo, wor = wload(dr["ssm_wout"][j, g, dt_], 512)
                        for tbh in range(2):
                            po, por = getps()
                            for kt in range(4):
                                S.op(pe, lambda e, po=po, wo=wo, kt=kt, tbh=tbh: e.matmul(
                                    po[:], lhsT=wo[:, kt * 128:(kt + 1) * 128], rhs=zs[:, kt, tbh * TB:(tbh + 1) * TB],
                                    start=(kt == 0), stop=(kt == 3)),
                                    reads=[wor] + zres[kt][tbh * 4:(tbh + 1) * 4], writes=[por])
                            htb = (T0 // TB) + tbh
                            S.op(dve, lambda e, po=po, dt_=dt_, htb=htb: e.tensor_tensor(
                                out=hv[:, dt_, htb * TB:(htb + 1) * TB], in0=po[:], in1=hv[:, dt_, htb * TB:(htb + 1) * TB],
                                op=ALU.add), reads=[por, hres[dt_][htb]], writes=[hres[dt_][htb]])

        tails = None
        wdt_t = None
        wdtr = Res()
        tailres = Res()
        ssm_state = None
        ssm_state_res = None
        if need_ssm:
            wdt_t = sb("wdt", [128, 256], BF16)
            tails_t = sb("tails", [128, 96], BF16)
            tails = tails_t[:, :]
            st_t = sb("sstate", [128, 4 * 512], F32)
            ssm_state = [st_t[:, g * 512:(g + 1) * 512] for g in range(4)]
            ssm_state_res = [Res() for _ in range(4)]

        def store_h(final):
            S.barrier()
            AR.reset()
            sq = [AR.bf(TB) for _ in range(2)]
            sqres = [Res(), Res()]
            rstd = [AR.f32(TB) for _ in range(2)]
            rstdres = [Res(), Res()]
            yn = [AR.f32(NKT * 128).rearrange("p (k t) -> p k t", k=NKT) for _ in range(2)]
            ynres = [Res(), Res()]
            ob = [AR.f32(D) for _ in range(2)]
            obres = [Res(), Res()]
            for tb in range(4):
                if final:
                    pt, pr = getps()
                    ts = slice(tb * TB, (tb + 1) * TB)
                    for kt in range(NKT):
                        b = kt % 2
                        S.op(act, lambda e, b=b, kt=kt, ts=ts: e.activation(out=sq[b], in_=hv[:, kt, ts], func=AF.Square,
                                                                            scale=float(D ** -0.5)),
                             reads=[hres[kt][tb]], writes=[sqres[b]])
                        S.op(pe, lambda e, pt=pt, b=b, kt=kt: e.matmul(pt[:], lhsT=ones_b, rhs=sq[b], start=(kt == 0),
                                                                       stop=(kt == NKT - 1)),
                             reads=[sqres[b], r_cbf], writes=[pr])
                    rb = tb % 2
                    S.op(act, lambda e, pt=pt, rb=rb: e.activation(out=rstd[rb], in_=pt[:], func=AF.Ln, bias=eps_col),
                         reads=[pr, r_cbf], writes=[rstdres[rb]])
                    S.op(act, lambda e, rb=rb: e.activation(out=rstd[rb], in_=rstd[rb], func=AF.Exp, scale=-0.5),
                         reads=[rstdres[rb]], writes=[rstdres[rb]])
                for t4 in range(4):
                    tt = tb * 4 + t4
                    b = tt % 2
                    tsl = slice(tt * 128, (tt + 1) * 128)
                    if final:
                        for kt in range(NKT):
                            S.op(dve, lambda e, b=b, kt=kt, tsl=tsl, rb=rb, t4=t4: e.scalar_tensor_tensor(
                                out=yn[b][:, kt, :], in0=hv[:, kt, tsl], scalar=pcolap("final_norm", kt),
                                in1=rstd[rb][:, t4 * 128:(t4 + 1) * 128], op0=ALU.mult, op1=ALU.mult),
                                reads=[hres[kt][tb], rstdres[rb], r_const], writes=[ynres[b]])
                    for half in range(2):
                        pt2, pr2 = getps()
                        for q in range(4):
                            kt = half * 4 + q
                            src = yn[b][:, kt, :] if final else hv[:, kt, tsl]
                            rd = [ynres[b]] if final else [hres[kt][tb]]
                            S.op(pe, lambda e, pt2=pt2, q=q, src=src: e.transpose(pt2[:, q * 128:(q + 1) * 128], src,
                                                                                  ident_f),
                                 reads=rd + [r_const], writes=[pr2])
                        dst = ob[b][:, half * 512:(half + 1) * 512]
                        if half == 0:
                            S.op(act, lambda e, dst=dst, pt2=pt2: e.copy(out=dst, in_=pt2[:]), reads=[pr2],
                                 writes=[obres[b]])
                        else:
                            S.op(dve, lambda e, dst=dst, pt2=pt2: e.tensor_copy(out=dst, in_=pt2[:]), reads=[pr2],
                                 writes=[obres[b]])
                    S.dma(sp, out_d[tt * 128:(tt + 1) * 128, :], ob[b], reads=[obres[b]], dres=obres[b])
            toks = [("dma", obres[0], obres[0].dcount * 16), ("dma", obres[1], obres[1].dcount * 16)]
            if debug:
                S.dma(sp, dbg_d, dbg_stage[:], reads=[dbgres], dres=dbgres)
                toks.append(("dma", dbgres, dbgres.dcount * 16))
            S.wait_all(sp, toks)

        load_h()
        for i in layers:
            if i % 2 == 0:
                mamba(i // 2)
            else:
                conformer(i // 2)
            ffn(i)
        store_h(last)
        S.emit(nc, stack)
    return nc


LAUNCH_PLAN = [[0, 1, 2, 3]]


def run_plan(inp, plan, n_cores=8, hin=None, debug=False):
    inp = {k: np.asarray(v, dtype=np.float32) for k, v in inp.items()}
    w = prep_weights(inp)
    pv = build_pv(inp)
    cst = build_consts()
    h = [np.ascontiguousarray(inp["x"][b]) for b in range(n_cores)] if hin is None else hin
    for li, layers in enumerate(plan):
        first = li == 0
        last = li == len(plan) - 1
        nc = build_program(layers, first, last, debug=debug)
        need_ssm = any(i % 2 == 0 for i in layers)
        need_cv = any(i % 2 == 1 for i in layers)
        base = {"cst": cst, "pv": pv, "bv": w["bv"], "ffn_wg": w["ffn_wg"], "ffn_wu": w["ffn_wu"], "ffn_wd": w["ffn_wd"]}
        if need_ssm:
            base.update({"ssm_win": w["ssm_win"], "ssm_wdt": w["ssm_wdt"], "ssm_wout": w["ssm_wout"]})
        if need_cv:
            base.update({"cv_w1": w["cv_w1"], "cv_w2": w["cv_w2"]})
        in_maps = []
        for b in range(n_cores):
            m = dict(base)
            m["hin"] = h[b]
            in_maps.append(m)
        res = run_bass_kernel_spmd(nc, in_maps, core_ids=list(range(n_cores)))
        h = [np.asarray(res.results[b]["hout"], dtype=np.float32) for b in range(n_cores)]
        if debug:
            return h, np.asarray(res.results[0]["dbg"])
    return h


def kernel(**inputs):
    h = run_plan(inputs, LAUNCH_PLAN, n_cores=8)
    return np.stack(h, axis=0).astype(np.float32)
```
